# Optimizing a Trainium2 kernel written in Bass

```python
import math
import jax, jax.numpy as jnp
from jax import lax
import numpy as np

D_MODEL = 1024
BATCH = 4
SEQ = 4096
DEPTH = 2
DEC_BATCH = 32
DEC_SEQ = 64
PAST_LEN = 2048

CHUNK = 64
D_MIX = D_MODEL
D_A = 3 * D_MIX // 8
D_B = D_MIX // 4
D_C = D_MIX - D_A - D_B
HGRN_HEAD = 64
HGRN_HEADS = D_A // HGRN_HEAD
HGRN_BLOCK = 16
S5_GROUP = 16
S5_GROUPS = D_B // S5_GROUP
S5_STATE = 64
LRU_BLOCK = 64
LRU_HEADS = D_C // LRU_BLOCK
CONV_W = 4
LRU_C = 8.0
D_FF = -(-8 * D_MODEL // (3 * 256)) * 256
D_IN = 4 * D_A + D_B + 2 * D_C
EPS = 1e-6

kernel_name = "hymba_hgrn2_s5_rglru_stream_step"

F32 = jnp.float32


def rmsnorm(x, w):
    x32 = x.astype(F32)
    y = x32 * lax.rsqrt(jnp.mean(x32 * x32, axis=-1, keepdims=True) + EPS)
    return (y * w.astype(F32)).astype(x.dtype)


def rms32(x, w):
    return x * lax.rsqrt(jnp.mean(x * x, axis=-1, keepdims=True) + EPS) * w.astype(F32)


def hgrn2_mixer(q, f_raw, inp, g, lb, norm_w, S0):
    Bsz, L, _ = q.shape
    q = q.astype(F32)
    f_raw = f_raw.astype(F32)
    inp = inp.astype(F32)
    f = lb + (1.0 - lb) * jax.nn.sigmoid(f_raw)
    log_f = jnp.log(f)
    k = (1.0 - lb) * jax.nn.sigmoid(-f_raw)
    nb = -(-L // HGRN_BLOCK)
    pad = nb * HGRN_BLOCK - L

    def blocks(t):
        t = jnp.pad(t, ((0, 0), (0, pad), (0, 0)))
        t = t.reshape(Bsz, nb, HGRN_BLOCK, HGRN_HEADS, HGRN_HEAD)
        return t.transpose(1, 0, 3, 2, 4)

    mask = jnp.tril(jnp.ones((HGRN_BLOCK, HGRN_BLOCK), dtype=bool))
    ref = HGRN_BLOCK // 2 - 1

    def step(S, blk):
        qb, kb, vb, gb = blk
        G = jnp.cumsum(gb, axis=2)
        Gr = G[:, :, ref:ref + 1]
        att = jnp.einsum('bhtd,bhsd->bhts', qb * jnp.exp(G - Gr), kb * jnp.exp(Gr - G))
        att = jnp.where(mask, att, 0.0)
        o = (jnp.einsum('bhts,bhsv->bhtv', att, vb)
             + jnp.einsum('bhtd,bhdv->bhtv', qb * jnp.exp(G), S))
        Gl = G[:, :, -1:]
        S = (jnp.exp(Gl)[:, :, 0, :, None] * S
             + jnp.einsum('bhsd,bhsv->bhdv', kb * jnp.exp(Gl - G), vb))
        return S, o

    S_fin, o = lax.scan(step, S0.astype(F32),
                        (blocks(q), blocks(k), blocks(inp), blocks(log_f)))
    o = o.transpose(1, 0, 3, 2, 4).reshape(Bsz, nb * HGRN_BLOCK, HGRN_HEADS, HGRN_HEAD)[:, :L]
    o = o * lax.rsqrt(jnp.mean(o * o, axis=-1, keepdims=True) + EPS)
    o = o.reshape(Bsz, L, D_A) * norm_w.astype(F32) * jax.nn.silu(g.astype(F32))
    return o, S_fin


def s5_mixer(u, lam_re, lam_im, log_dt, b_re, b_im, c_re, c_im, d, w_glu, b_glu, norm_w,
             h0_re, h0_im):
    Bsz, L, _ = u.shape
    u = u.astype(F32)
    ug = u.reshape(Bsz, L, S5_GROUPS, S5_GROUP)
    dt = jnp.exp(log_dt.astype(F32))[:, None]
    lr = jnp.minimum(lam_re.astype(F32), -1e-4)
    li = lam_im.astype(F32)
    mag = jnp.exp(lr * dt)
    a_re = mag * jnp.cos(li * dt)
    a_im = mag * jnp.sin(li * dt)
    den = lr * lr + li * li
    nr = a_re - 1.0
    gam_re = (nr * lr + a_im * li) / den
    gam_im = (a_im * lr - nr * li) / den
    b_re = b_re.astype(F32)
    b_im = b_im.astype(F32)
    bbar_re = gam_re[..., None] * b_re - gam_im[..., None] * b_im
    bbar_im = gam_re[..., None] * b_im + gam_im[..., None] * b_re
    bu_re = jnp.einsum('blgn,gpn->blgp', ug, bbar_re)
    bu_im = jnp.einsum('blgn,gpn->blgp', ug, bbar_im)
    h0_re = h0_re.astype(F32)
    h0_im = h0_im.astype(F32)
    bu_re = bu_re.at[:, 0].add(a_re * h0_re - a_im * h0_im)
    bu_im = bu_im.at[:, 0].add(a_re * h0_im + a_im * h0_re)
    ar = jnp.broadcast_to(a_re, bu_re.shape)
    ai = jnp.broadcast_to(a_im, bu_re.shape)

    def combine(e1, e2):
        a1r, a1i, b1r, b1i = e1
        a2r, a2i, b2r, b2i = e2
        return (a2r * a1r - a2i * a1i, a2r * a1i + a2i * a1r,
                a2r * b1r - a2i * b1i + b2r, a2r * b1i + a2i * b1r + b2i)

    _, _, h_re, h_im = lax.associative_scan(combine, (ar, ai, bu_re, bu_im), axis=1)
    y = (jnp.einsum('blgp,gnp->blgn', h_re, c_re.astype(F32))
         - jnp.einsum('blgp,gnp->blgn', h_im, c_im.astype(F32)))
    y = y.reshape(Bsz, L, D_B) + d.astype(F32) * u
    z = jax.nn.gelu(y)
    z = z * jax.nn.sigmoid(z @ w_glu.astype(F32) + b_glu.astype(F32))
    return rms32(z, norm_w), h_re[:, -1], h_im[:, -1]


def rglru_mixer(xb, gate, conv_w, conv_b, wa, ba, wx, bx, lam, norm_w, h0, buf):
    Bsz, L, _ = xb.shape
    xp = jnp.concatenate([buf.astype(F32), xb.astype(F32)], axis=1)
    conv_w = conv_w.astype(F32)
    xc = conv_b.astype(F32) + sum(xp[:, j:j + L] * conv_w[j] for j in range(CONV_W))
    new_buf = xp[:, L:]
    xh = xc.reshape(Bsz, L, LRU_HEADS, LRU_BLOCK)
    r = jax.nn.sigmoid(jnp.einsum('blhi,hij->blhj', xh, wa.astype(F32)).reshape(Bsz, L, D_C)
                       + ba.astype(F32))
    i = jax.nn.sigmoid(jnp.einsum('blhi,hij->blhj', xh, wx.astype(F32)).reshape(Bsz, L, D_C)
                       + bx.astype(F32))
    log_a = -LRU_C * r * jax.nn.softplus(-lam.astype(F32))
    a = jnp.exp(log_a)
    b = jnp.sqrt(-jnp.expm1(2.0 * log_a)) * (i * xc)
    b = b.at[:, 0].add(a[:, 0] * h0.astype(F32))

    def combine(e1, e2):
        return (e2[0] * e1[0], e2[0] * e1[1] + e2[1])

    _, h = lax.associative_scan(combine, (a, b), axis=1)
    y = h * jax.nn.gelu(gate.astype(F32))
    return rms32(y, norm_w), h[:, -1], new_buf


def trunk(x, s_hgrn, s_s5r, s_s5i, s_lru, s_conv, P):
    sm = jax.nn.softmax(P['hgrn_lb'].astype(F32), axis=0)
    cs = jnp.cumsum(sm, axis=0)
    lbs = cs - cs[0:1]
    splits = [D_A, 2 * D_A, 3 * D_A, 4 * D_A, 4 * D_A + D_B, 4 * D_A + D_B + D_C]
    out_h, out_sr, out_si, out_l, out_c = [], [], [], [], []
    for l in range(DEPTH):
        xn = rmsnorm(x, P['norm_mix'][l])
        proj = jnp.einsum('bld,de->ble', xn, P['w_in'][l])
        q, f_raw, inp, g, u, xr, gr = jnp.split(proj, splits, axis=-1)
        oA, hA = hgrn2_mixer(q, f_raw, inp, g, lbs[l], P['hgrn_norm'][l], s_hgrn[l])
        oB, hBr, hBi = s5_mixer(u, P['s5_lam_re'][l], P['s5_lam_im'][l], P['s5_log_dt'][l],
                                P['s5_b_re'][l], P['s5_b_im'][l], P['s5_c_re'][l],
                                P['s5_c_im'][l], P['s5_d'][l], P['s5_w_glu'][l],
                                P['s5_b_glu'][l], P['s5_norm'][l], s_s5r[l], s_s5i[l])
        oC, hC, bufC = rglru_mixer(xr, gr, P['lru_conv_w'][l], P['lru_conv_b'][l],
                                   P['lru_wa'][l], P['lru_ba'][l], P['lru_wx'][l],
                                   P['lru_bx'][l], P['lru_lam'][l], P['lru_norm'][l],
                                   s_lru[l], s_conv[l])
        mix = jnp.concatenate([oA, oB, oC], axis=-1).astype(x.dtype)
        x = x + jnp.einsum('ble,ed->bld', mix, P['w_out'][l])
        xn2 = rmsnorm(x, P['norm_ffn'][l])
        hid = (jax.nn.silu(jnp.einsum('bld,df->blf', xn2, P['w_ffn_gate'][l]))
               * jnp.einsum('bld,df->blf', xn2, P['w_ffn_up'][l]))
        x = x + jnp.einsum('blf,fd->bld', hid, P['w_ffn_down'][l])
        out_h.append(hA); out_sr.append(hBr); out_si.append(hBi)
        out_l.append(hC); out_c.append(bufC)
    y = rmsnorm(x, P['norm_final'])
    return (y, jnp.stack(out_h), jnp.stack(out_sr), jnp.stack(out_si),
            jnp.stack(out_l), jnp.stack(out_c))


def setup_inputs(seed: int = 0) -> dict:
    key = jax.random.key(seed)
    ks = iter(jax.random.split(key, 48))

    def nrm(shape, scale):
        return scale * jax.random.normal(next(ks), shape, F32)

    n = jnp.arange(S5_STATE, dtype=F32)
    u_lam = jax.random.uniform(next(ks), (DEPTH, D_C), F32, 0.9, 0.999)
    s_lam = u_lam ** (1.0 / LRU_C)
    return {
        "x_prompt": nrm((BATCH, SEQ, D_MODEL), 1.0),
        "x_sample": nrm((DEC_BATCH, DEC_SEQ, D_MODEL), 1.0),
        "state_hgrn": nrm((DEPTH, DEC_BATCH, HGRN_HEADS, HGRN_HEAD, HGRN_HEAD), 0.5),
        "state_s5_re": nrm((DEPTH, DEC_BATCH, S5_GROUPS, S5_STATE), 0.5),
        "state_s5_im": nrm((DEPTH, DEC_BATCH, S5_GROUPS, S5_STATE), 0.5),
        "state_rglru": nrm((DEPTH, DEC_BATCH, D_C), 1.0),
        "cache_conv": nrm((DEPTH, DEC_BATCH, CONV_W - 1, D_C), 1.0),
        "norm_mix": 1.0 + nrm((DEPTH, D_MODEL), 0.02),
        "w_in": nrm((DEPTH, D_MODEL, D_IN), D_MODEL ** -0.5),
        "hgrn_lb": nrm((DEPTH, D_A), 1.0),
        "hgrn_norm": 1.0 + nrm((DEPTH, D_A), 0.02),
        "s5_lam_re": -0.5 + nrm((DEPTH, S5_GROUPS, S5_STATE), 0.01),
        "s5_lam_im": jnp.pi * n + nrm((DEPTH, S5_GROUPS, S5_STATE), 0.01),
        "s5_log_dt": jax.random.uniform(next(ks), (DEPTH, S5_GROUPS), F32,
                                        math.log(1e-3), math.log(1e-1)),
        "s5_b_re": nrm((DEPTH, S5_GROUPS, S5_STATE, S5_GROUP), (2 * S5_GROUP) ** -0.5),
        "s5_b_im": nrm((DEPTH, S5_GROUPS, S5_STATE, S5_GROUP), (2 * S5_GROUP) ** -0.5),
        "s5_c_re": nrm((DEPTH, S5_GROUPS, S5_GROUP, S5_STATE), (2 * S5_STATE) ** -0.5),
        "s5_c_im": nrm((DEPTH, S5_GROUPS, S5_GROUP, S5_STATE), (2 * S5_STATE) ** -0.5),
        "s5_d": nrm((DEPTH, D_B), 1.0),
        "s5_w_glu": nrm((DEPTH, D_B, D_B), D_B ** -0.5),
        "s5_b_glu": nrm((DEPTH, D_B), 0.01),
        "s5_norm": 1.0 + nrm((DEPTH, D_B), 0.02),
        "lru_conv_w": nrm((DEPTH, CONV_W, D_C), CONV_W ** -0.5),
        "lru_conv_b": nrm((DEPTH, D_C), 0.01),
        "lru_wa": nrm((DEPTH, LRU_HEADS, LRU_BLOCK, LRU_BLOCK), LRU_BLOCK ** -0.5),
        "lru_ba": nrm((DEPTH, D_C), 0.01),
        "lru_wx": nrm((DEPTH, LRU_HEADS, LRU_BLOCK, LRU_BLOCK), LRU_BLOCK ** -0.5),
        "lru_bx": nrm((DEPTH, D_C), 0.01),
        "lru_lam": jnp.log(s_lam) - jnp.log1p(-s_lam),
        "lru_norm": 1.0 + nrm((DEPTH, D_C), 0.02),
        "w_out": nrm((DEPTH, D_MIX, D_MODEL), D_MIX ** -0.5),
        "norm_ffn": 1.0 + nrm((DEPTH, D_MODEL), 0.02),
        "w_ffn_gate": nrm((DEPTH, D_MODEL, D_FF), D_MODEL ** -0.5),
        "w_ffn_up": nrm((DEPTH, D_MODEL, D_FF), D_MODEL ** -0.5),
        "w_ffn_down": nrm((DEPTH, D_FF, D_MODEL), D_FF ** -0.5),
        "norm_final": 1.0 + nrm((D_MODEL,), 0.02),
    }


def reference(x_prompt, x_sample, state_hgrn, state_s5_re, state_s5_im, state_rglru, cache_conv,
              norm_mix, w_in, hgrn_lb, hgrn_norm, s5_lam_re, s5_lam_im, s5_log_dt, s5_b_re,
              s5_b_im, s5_c_re, s5_c_im, s5_d, s5_w_glu, s5_b_glu, s5_norm, lru_conv_w,
              lru_conv_b, lru_wa, lru_ba, lru_wx, lru_bx, lru_lam, lru_norm, w_out, norm_ffn,
              w_ffn_gate, w_ffn_up, w_ffn_down, norm_final):
    P = {
        'norm_mix': norm_mix, 'w_in': w_in, 'hgrn_lb': hgrn_lb, 'hgrn_norm': hgrn_norm,
        's5_lam_re': s5_lam_re, 's5_lam_im': s5_lam_im, 's5_log_dt': s5_log_dt,
        's5_b_re': s5_b_re, 's5_b_im': s5_b_im, 's5_c_re': s5_c_re, 's5_c_im': s5_c_im,
        's5_d': s5_d, 's5_w_glu': s5_w_glu, 's5_b_glu': s5_b_glu, 's5_norm': s5_norm,
        'lru_conv_w': lru_conv_w, 'lru_conv_b': lru_conv_b, 'lru_wa': lru_wa, 'lru_ba': lru_ba,
        'lru_wx': lru_wx, 'lru_bx': lru_bx, 'lru_lam': lru_lam, 'lru_norm': lru_norm,
        'w_out': w_out, 'norm_ffn': norm_ffn, 'w_ffn_gate': w_ffn_gate, 'w_ffn_up': w_ffn_up,
        'w_ffn_down': w_ffn_down, 'norm_final': norm_final,
    }
    bp = x_prompt.shape[0]
    z_hgrn = jnp.zeros((DEPTH, bp, HGRN_HEADS, HGRN_HEAD, HGRN_HEAD), F32)
    z_s5 = jnp.zeros((DEPTH, bp, S5_GROUPS, S5_STATE), F32)
    z_lru = jnp.zeros((DEPTH, bp, D_C), F32)
    z_conv = jnp.zeros((DEPTH, bp, CONV_W - 1, D_C), F32)
    y_prompt, hgrn_p, s5re_p, s5im_p, lru_p, conv_p = trunk(
        x_prompt, z_hgrn, z_s5, z_s5, z_lru, z_conv, P)
    y_sample, hgrn_s, s5re_s, s5im_s, lru_s, conv_s = trunk(
        x_sample, state_hgrn, state_s5_re, state_s5_im, state_rglru, cache_conv, P)
    return (y_prompt, y_sample, hgrn_p, s5re_p, s5im_p, lru_p, conv_p,
            hgrn_s, s5re_s, s5im_s, lru_s, conv_s)
```

```python
import contextlib
import numpy as np
import concourse.bass as bass
import concourse.mybir as mybir
from concourse.bass_utils import run_bass_kernel_spmd

F32 = mybir.dt.float32
BF16 = mybir.dt.bfloat16
AF = mybir.ActivationFunctionType
ALU = mybir.AluOpType

D = 1024
DEPTH = 2
DA, DB, DC = 384, 256, 384
DIN = 2560
DFF = 2816
NFF = DFF // 128
EPS = 1e-6
CH = 512
NS = 8
LS = 64
SEQ = 4096
NCORE = 8
PI = float(np.pi)

ENGS = ("tensor", "vector", "scalar", "gpsimd", "sync")
SEM_ROT = 2000


class Buf:
    def __init__(self, name, h, group=None):
        self.name = name
        self.h = h
        self.lw = []
        self.rd = {}
        self.group = group
        self.psum = False

    def __getitem__(self, k):
        return self.h[k]


class DmaGroup:
    def __init__(self, sem):
        self.sem = sem
        self.count = 0


class Prog:
    def __init__(self, nc, stack):
        self.nc = nc
        self.stack = stack
        self.streams = {e: [] for e in ENGS}
        self.esem = {}
        self.ecnt = {e: 0 for e in ENGS}
        self.known = {e: {} for e in ENGS}
        self.nsem = 0
        self.groups = []
        self.nbuf = 0
        self.pend = {e: ([], []) for e in ENGS}
        self.ninst = {e: 0 for e in ENGS}
        self.sem2group = {}
        self.ccsem = None
        self.cccnt = 0
        for e in ENGS:
            self.esem[e] = self.new_sem("e_" + e)

    def new_sem(self, name):
        self.nsem += 1
        return self.stack.enter_context(self.nc.semaphore(f"{name}_{self.nsem}"))

    def new_group(self, name="dg"):
        g = DmaGroup(self.new_sem(name))
        self.groups.append(g)
        self.sem2group[id(g.sem)] = g
        return g

    def sb(self, name, shape, dt=F32, group=None):
        self.nbuf += 1
        h = self.stack.enter_context(self.nc.sbuf_tensor(f"{name}_{self.nbuf}", list(shape), dt))
        return Buf(name, h, group)

    def ps(self, name, shape, dt=F32):
        self.nbuf += 1
        h = self.stack.enter_context(self.nc.psum_tensor(f"{name}_{self.nbuf}", list(shape), dt))
        b = Buf(name, h)
        b.psum = True
        return b

    def dram(self, name, shape, dt=F32, kind="Internal", **kw):
        t = self.nc.dram_tensor(name, list(shape), dt, kind=kind, **kw)
        return Buf(name, t.ap())

    def _need(self, eng, evs):
        kn = self.known[eng]
        for (sem, val, _e) in evs:
            if _e == "dma":
                val = 16 * self.sem2group[id(sem)].count
            if kn.get(id(sem), (None, 0))[1] >= val:
                continue
            kn[id(sem)] = (sem, val)
            self.streams[eng].append(("wait", sem, val))

    def _deps(self, r, w, eng=None):
        evs = []
        for b in r:
            evs.extend(b.lw)
            if b.psum:
                evs.extend(ev for ev in b.rd.values() if ev[2] != eng)
        for b in w:
            evs.extend(b.lw)
            evs.extend(b.rd.values())
        return evs

    def _commit(self, ev, r, w):
        k = id(ev[0])
        for b in r:
            if k not in b.rd or b.rd[k][1] < ev[1]:
                b.rd[k] = ev
        for b in w:
            b.lw = [ev]
            b.rd = {}

    def pe_fence(self):
        if self.ecnt["tensor"]:
            self._need("tensor", [(self.esem["tensor"], self.ecnt["tensor"], "tensor")])

    def op(self, eng, fn, r=(), w=(), inc=True):
        self._need(eng, self._deps(r, w, eng))
        self.ninst[eng] += 1
        if inc:
            if self.ecnt[eng] >= SEM_ROT:
                self.esem[eng] = self.new_sem("e_" + eng)
                self.ecnt[eng] = 0
            self.ecnt[eng] += 1
            ev = (self.esem[eng], self.ecnt[eng], eng)
            self.streams[eng].append(("inst", fn, ev[0], 1))
            pr, pw = self.pend[eng]
            self._commit(ev, list(r) + pr, list(w) + pw)
            self.pend[eng] = ([], [])
        else:
            self.streams[eng].append(("inst", fn, None, 0))
            self.pend[eng][0].extend(r)
            self.pend[eng][1].extend(w)

    def dma(self, eng, out_ap, in_ap, src, dst, primary=None, **kw):
        src = list(src) if isinstance(src, (list, tuple)) else [src]
        dst = list(dst) if isinstance(dst, (list, tuple)) else [dst]
        self._need(eng, self._deps(src, dst, "dma"))
        p = primary
        if p is None:
            p = dst[0] if dst[0].group is not None else src[0]
        g = p.group
        assert g is not None, f"dma needs a group on {src[0].name}/{dst[0].name}"
        g.count += 1
        ev = (g.sem, 16 * g.count, "dma")
        self.streams[eng].append(
            ("inst", lambda e, o=out_ap, i=in_ap, kw=kw: e.dma_start(out=o, in_=i, **kw), g.sem, 16))
        self._commit(ev, src, dst)

    def coll(self, fn, r, w):
        eng = "gpsimd"
        if self.ccsem is None:
            self.ccsem = self.new_sem("cc")
        self._need(eng, self._deps(r, w, "cc"))
        self.cccnt += 1
        ev = (self.ccsem, self.cccnt, "cc")
        self.streams[eng].append(("inst", fn, self.ccsem, 1))
        self._commit(ev, list(r), list(w))

    def finish(self, eng="sync"):
        evs = [(g.sem, 16 * g.count, "dma") for g in self.groups if g.count]
        for e in ENGS:
            if self.ecnt[e]:
                evs.append((self.esem[e], self.ecnt[e], e))
        if self.cccnt:
            evs.append((self.ccsem, self.cccnt, "cc"))
        self._need(eng, evs)

    def emit(self):
        with self.nc.Block() as block:
            for e in ENGS:
                stream = self.streams[e]
                if not stream:
                    continue

                def body(engine, stream=stream):
                    for it in stream:
                        if it[0] == "wait":
                            engine.wait_ge(it[1], it[2])
                        else:
                            ins = it[1](engine)
                            if it[2] is not None:
                                ins.then_inc(it[2], it[3])

                getattr(block, e)(body)


class StopBuild(Exception):
    pass


class Pool:
    def __init__(self, bufs):
        self.free = list(bufs)

    def get(self):
        assert self.free, "pool exhausted"
        return self.free.pop(0)

    def put(self, *bs):
        for b in bs:
            assert b not in self.free
            self.free.append(b)


def _col_layout():
    off = {}
    n = 0

    def add(name, w):
        nonlocal n
        off[name] = n
        n += w
    for l in range(DEPTH):
        add(f"norm_mix{l}", 8)
        add(f"norm_ffn{l}", 8)
        add(f"hgrn_lb{l}", 3)
        add(f"hgrn_norm{l}", 3)
        add(f"s5_d{l}", 2)
        add(f"s5_bglu{l}", 2)
        add(f"s5_norm{l}", 2)
        for j in range(4):
            add(f"conv_w{l}_{j}", 3)
        add(f"conv_b{l}", 3)
        add(f"lru_ba{l}", 3)
        add(f"lru_bx{l}", 3)
        add(f"lru_lam{l}", 3)
        add(f"lru_norm{l}", 3)
        add(f"s5_lre{l}", 8)
        add(f"s5_lim{l}", 8)
        add(f"s5_ldt{l}", 8)
    add("norm_final", 8)
    add("role", 1)
    add("m_own", 1)
    return off, n


COLS, NCOL = _col_layout()


def build_program(TP, cfg=None):
    cfg = cfg or {}
    NDEPTH = cfg.get("depth", DEPTH)
    NPC = TP // CH
    NST = NPC + 1
    TTX = NST * CH + NS * LS
    TTY = (NST + 2) * CH
    nc = bass.Bass("TRN2", target_bir_lowering=False)
    with contextlib.ExitStack() as stack:
        P = Prog(nc, stack)
        V, A, T, G = "vector", "scalar", "tensor", "gpsimd"

        def din(name, shape, dt=F32):
            return P.dram(name, shape, dt, kind="ExternalInput")

        def dout(name, shape, dt=F32):
            return P.dram(name, shape, dt, kind="ExternalOutput")

        xT = din("xT", [8, 128, TTX])
        st_hgrn = din("st_hgrn", [DEPTH, 3, 128, NS, 128])
        st_s5 = din("st_s5", [DEPTH, 128, 8, NS, 2])
        st_lru = din("st_lru", [DEPTH, 128, 3, NS])
        st_conv = din("st_conv", [DEPTH, 128, 3, NS, 3])
        w_in = din("w_in", [1, D, DIN])
        w_out = din("w_out", [1, D, D])
        w_g = din("w_g", [1, D, DFF])
        w_u = din("w_u", [1, D, DFF])
        w_d = din("w_d", [1, DFF, D])
        cols_d = din("cols", [128, NCOL])
        s5rep_d = din("s5rep", [DEPTH, 3, 128, 1024])
        s5b_d = din("s5b", [DEPTH, 2, 128, 1024])
        s5c_d = din("s5c", [DEPTH, 2, 128, 1024])
        wglu_d = din("wglu", [DEPTH, 128, 2, 256])
        lruw_d = din("lruw", [DEPTH, 2, 128, 3, 128])
        cmat_d = din("cmat", [128, 3, 128])
        cmask_d = din("cmask", [128, 64])
        crm_d = din("crm", [128, CH])

        y_raw = dout("y_raw", [8, 128, TTY])
        hgrnP = dout("hgrnP", [DEPTH, 3, 128, 128])
        hgrnS = dout("hgrnS", [DEPTH, 3, 128, NS, 128])
        s5P = dout("s5P", [DEPTH, 128, 8, 2])
        s5S = dout("s5S", [DEPTH, 128, 8, NS, 2])
        lruP = dout("lruP", [DEPTH, 128, 3])
        lruS = dout("lruS", [DEPTH, 128, 3, NS])
        convP = dout("convP", [DEPTH, 128, 3, 3])
        convS = dout("convS", [DEPTH, 128, 3, NS, 3])

        win_s = [P.dram(f"win_s{l}", [128, 8, DIN], BF16) for l in range(DEPTH)]
        wout_s = [P.dram(f"wout_s{l}", [128, 8, D], BF16) for l in range(DEPTH)]
        wg_s = [P.dram(f"wg_s{l}", [128, 8, DFF], BF16) for l in range(DEPTH)]
        wu_s = [P.dram(f"wu_s{l}", [128, 8, DFF], BF16) for l in range(DEPTH)]
        wd_s = [P.dram(f"wd_s{l}", [128, NFF, D], BF16) for l in range(DEPTH)]
        for t in win_s + wout_s + wg_s + wu_s + wd_s:
            t.group = P.new_group("wcast")
        send = P.dram("send", [128, 8 * CH], F32, addr_space="Local")
        recv = P.dram("recv", [256, 8 * CH], F32, addr_space="Local")
        grecv = P.new_group("recv")

        def cast_weights(l, which):
            for (src, dst, nk) in (((w_in, win_s, 8),) if which == 0 else
                                   ((w_out, wout_s, 8), (w_g, wg_s, 8), (w_u, wu_s, 8), (w_d, wd_s, NFF))):
                sv = src[l].rearrange("(kt p) f -> p kt f", p=128)
                for kt in range(nk):
                    P.dma(G, dst[l][:, kt, :], sv[:, kt, :], src, dst[l])

        gc = P.new_group("const")
        cols = P.sb("cols", [128, NCOL], F32, group=gc)
        cmat = P.sb("cmat", [128, 3, 128], BF16)
        MASK = P.sb("mask", [128, 64], F32, group=gc)
        RM = P.sb("rm", [128, CH], BF16, group=gc)
        P.dma("sync", cols[:], cols_d[:], cols_d, cols)
        P.dma("sync", MASK[:], cmask_d[:], cmask_d, MASK)
        P.dma("gpsimd", RM[:], crm_d[:], crm_d, RM)
        ONES = cmat[:, 0, :]
        BD64 = cmat[:, 1, :]
        IDENT = cmat[:, 2, :]

        def col(name, i=0, n=1):
            o = COLS[name] + i
            return cols[:, o:o + n]

        NF32 = cfg.get("nf32", 29)
        NB16 = cfg.get("nb16", 31)
        f32pool = Pool([P.sb(f"w{i}", [128, CH], F32) for i in range(NF32)])
        b16pool = Pool([P.sb(f"h{i}", [128, CH], BF16) for i in range(NB16)])
        cm32 = f32pool.get()
        cm32.group = gc
        P.dma("sync", cm32[:, 0:384].rearrange("p (a m) -> p a m", m=128), cmat_d[:], cmat_d, cm32)
        P.op(V, lambda e: e.tensor_copy(out=cmat[:], in_=cm32[:, 0:384].rearrange("p (a m) -> p a m", m=128)), r=[cm32], w=[cmat])
        cm32.group = None
        f32pool.put(cm32)
        banks = [P.ps(f"pb{i}", [128, CH], F32) for i in range(7)]
        psmm = Pool(banks[0:4])
        ps_hg = Pool(banks[0:2])
        ps_s5 = Pool(banks[2:5])
        ps_lru = Pool(banks[5:7])
        pst = P.ps("pst", [128, 1024], BF16)

        gx = P.new_group("x")
        Xall = stack.enter_context(nc.sbuf_tensor("Xall", [128, 8, CH], F32))
        X = [Buf(f"X{k}", Xall[:, k, :], group=gx) for k in range(8)]
        XRH = [P.sb(f"xrh{i}", [128, NS * (LS + 3)], F32) for i in range(3)]
        NSLOT = 4
        ring = [P.sb(f"ring{i}", [128, 8 * 512], BF16, group=P.new_group(f"ring{i}")) for i in range(NSLOT)]

        BRE = P.sb("bre", [128, 8, 128], BF16)
        BIM = P.sb("bim", [128, 8, 128], BF16)
        CRE = P.sb("cre", [128, 8, 128], BF16)
        CIMN = P.sb("cimn", [128, 8, 128], BF16)
        COS = P.sb("cos", [128, 8, CH], F32)
        SIN = P.sb("sin", [128, 8, CH], F32)
        WGLU = P.sb("wglu", [128, 2, 256], BF16)
        LRUW = P.sb("lruw", [128, 2, 3, 128], F32, group=P.new_group("lruw"))
        gwglu = P.new_group("wglu")
        SC = P.sb("sc", [128, 64], F32)
        SST = [P.sb(f"sst{i}", [128, 9, 128], F32) for i in range(3)]
        for i in range(3):
            P.op("vector", lambda e, i=i: e.memset(SST[i][:], 0.0), w=[SST[i]])
        gst = P.new_group("stin")
        gso = P.new_group("stout")
        for i in range(3):
            SST[i].group = gst
        SIN_H = SST
        SOUT_H = SIN_H
        H5 = P.sb("h5", [128, 8, NS, 2], F32, group=gst)
        H5O = P.sb("h5o", [128, 8, NS, 2], F32, group=P.new_group("h5o"))

        class _GH:
            def __init__(self, g):
                self.group = g
        GHG = [_GH(P.new_group(f"hgo{i}")) for i in range(3)]
        DECT = [P.sb(f"dec{i}", [128, 8], F32) for i in range(3)]
        G5 = P.sb("g5", [128, 8, NS, 2], F32)
        HL = P.sb("hl", [128, 3, NS], F32, group=gst)
        HLO = P.sb("hlo", [128, 3, NS], F32, group=P.new_group("hlo"))
        CVI = P.sb("cvi", [128, 3, NS, 3], F32, group=gst)
        CVO = P.sb("cvo", [128, 3, NS, 3], F32, group=P.new_group("cvo"))

        def act(out, in_, func, r, w, **kw):
            P.op(A, lambda e: e.activation(out=out, in_=in_, func=func, **kw), r=r, w=w)

        def tt(eng, out, in0, in1, op, r, w):
            P.op(eng, lambda e: e.tensor_tensor(out=out, in0=in0, in1=in1, op=op), r=r, w=w)

        def ts(eng, out, in0, s1, s2, op0, op1, r, w):
            if op1 is None:
                P.op(eng, lambda e: e.tensor_scalar(out=out, in0=in0, scalar1=s1, scalar2=None, op0=op0), r=r, w=w)
            else:
                P.op(eng, lambda e: e.tensor_scalar(out=out, in0=in0, scalar1=s1, scalar2=s2, op0=op0, op1=op1), r=r, w=w)

        def stt(out, in0, sc, in1, op0, op1, r, w):
            P.op(V, lambda e: e.scalar_tensor_tensor(out=out, in0=in0, scalar=sc, in1=in1, op0=op0, op1=op1), r=r, w=w)

        def cp(eng, out, in_, r, w):
            if eng == A:
                act(out, in_, AF.Copy, r, w)
            else:
                P.op(eng, lambda e: e.tensor_copy(out=out, in_=in_), r=r, w=w)

        def mm(out, lhsT, rhs, r, w, start=True, stop=True, inc=True):
            P.op(T, lambda e: e.matmul(out, lhsT, rhs, start=start, stop=stop, skip_group_check=True), r=r, w=w, inc=inc)

        def scan(out, d0, d1, init, r, w):
            P.op(V, lambda e: e.tensor_tensor_scan(out=out, data0=d0, data1=d1, initial=init, op0=ALU.mult, op1=ALU.add), r=r, w=w)

        def rstd_from(ps, n, out, r_extra=()):
            tmp = f32pool.get()
            act(tmp[:], ps[:], AF.Ln, [ps], [tmp], scale=1.0 / n, bias=EPSC[:, 0:1])
            act(out[:], tmp[:], AF.Exp, [tmp], [out], scale=-0.5)
            f32pool.put(tmp)

        EPSC = P.sb("epsc", [128, 2], F32)
        P.op(V, lambda e: e.memset(EPSC[:, 0:1], EPS), w=[EPSC])
        P.op(V, lambda e: e.memset(EPSC[:, 1:2], 1.0), w=[EPSC])

        panels = []
        for l in range(1):
            for c in range(NST + 2):
                for j in range(5):
                    panels.append((win_s[l], win_s[l][:, :, j * 512:(j + 1) * 512], (8, 512)))
                for j in range(2):
                    panels.append((wout_s[l], wout_s[l][:, :, j * 512:(j + 1) * 512], (8, 512)))
                for j in range(6):
                    wdt = 512 if j < 5 else 256
                    panels.append((wg_s[l], wg_s[l][:, :, j * 512:j * 512 + wdt], (8, wdt)))
                    panels.append((wu_s[l], wu_s[l][:, :, j * 512:j * 512 + wdt], (8, wdt)))
                for j in range(8):
                    panels.append((wd_s[l], wd_s[l][:, :, j * 128:(j + 1) * 128], (NFF, 128)))
        pstate = {"loaded": 0, "next": 0}

        def panel_view(slot, shp):
            k, wd = shp
            return slot[:, 0:k * wd].rearrange("p (k w) -> p k w", w=wd)

        def load_panel():
            i = pstate["loaded"]
            if i >= len(panels):
                return
            src, ap, shp = panels[i]
            slot = ring[i % NSLOT]
            P.dma("sync", panel_view(slot, shp), ap, src, slot)
            pstate["loaded"] += 1

        def get_panel(hold=0):
            i = pstate["next"]
            while pstate["loaded"] < min(i + NSLOT - hold, len(panels)):
                load_panel()
            pstate["next"] += 1
            src, ap, shp = panels[i]
            slot = ring[i % NSLOT]
            return slot, panel_view(slot, shp)

        def wrap_pi(t, tmp, w_):
            for _ in range(7):
                ts(V, tmp, t, PI, 2 * PI, ALU.is_gt, ALU.mult, [w_[0], w_[1]], [w_[1]])
                tt(V, t, t, tmp, ALU.subtract, [w_[0], w_[1]], [w_[0]])

        def s5_a(lre, lim, ldt, W, want_gam=False, tiles=None):
            t = {k: f32pool.get() for k in ("dt", "lr", "mag", "c", "s", "tmp", "ang")}
            rb = tiles
            def v(k):
                return t[k][:, 0:W]
            act(v("dt"), ldt, AF.Exp, rb, [t["dt"]])
            ts(V, v("lr"), lre, -1e-4, None, ALU.min, None, rb, [t["lr"]])
            tt(V, v("tmp"), v("lr"), v("dt"), ALU.mult, [t["lr"], t["dt"]], [t["tmp"]])
            act(v("mag"), v("tmp"), AF.Exp, [t["tmp"]], [t["mag"]])
            tt(V, v("ang"), lim, v("dt"), ALU.mult, rb + [t["dt"]], [t["ang"]])
            cp(V, v("s"), v("ang"), [t["ang"]], [t["s"]])
            wrap_pi(v("s"), v("tmp"), (t["s"], t["tmp"]))
            act(v("s"), v("s"), AF.Sin, [t["s"]], [t["s"]])
            ts(V, v("c"), v("ang"), PI / 2, None, ALU.add, None, [t["ang"]], [t["c"]])
            wrap_pi(v("c"), v("tmp"), (t["c"], t["tmp"]))
            act(v("c"), v("c"), AF.Sin, [t["c"]], [t["c"]])
            return t

        def layer_setup(l):
            P.dma("sync", LRUW[:], lruw_d[l].rearrange("a p i m -> p a i m"), lruw_d, LRUW)
            wg32 = f32pool.get()
            wg32.group = gwglu
            P.dma("sync", wg32[:].rearrange("p (a m) -> p a m", m=256), wglu_d[l], wglu_d, wg32)
            cp(V, WGLU[:], wg32[:].rearrange("p (a m) -> p a m", m=256), [wg32], [WGLU])
            wg32.group = None
            f32pool.put(wg32)
            tt(V, SC[:, 0:3], col("hgrn_lb1", 0, 3), col("hgrn_lb0", 0, 3), ALU.subtract, [cols], [SC])
            act(SC[:, 0:3], SC[:, 0:3], AF.Sigmoid, [SC], [SC])
            ts(V, SC[:, 0:3], SC[:, 0:3], col("role"), None, ALU.mult, None, [SC, cols], [SC])
            ts(V, SC[:, 3:6], SC[:, 0:3], -1.0, 1.0, ALU.mult, ALU.add, [SC], [SC])
            act(SC[:, 6:9], col(f"lru_lam{l}", 0, 3), AF.Exp, [cols], [SC], scale=-1.0)
            act(SC[:, 6:9], SC[:, 6:9], AF.Ln, [SC], [SC], bias=EPSC[:, 1:2])
            ts(V, SC[:, 6:9], SC[:, 6:9], -8.0, None, ALU.mult, None, [SC], [SC])
            t = s5_a(col(f"s5_lre{l}", 0, 8), col(f"s5_lim{l}", 0, 8), col(f"s5_ldt{l}", 0, 8), 8, tiles=[cols])
            cp(V, SC[:, 9:17], t["mag"][:, 0:8], [t["mag"]], [SC])
            cp(V, SC[:, 17:25], t["c"][:, 0:8], [t["c"]], [SC])
            cp(V, SC[:, 25:33], t["s"][:, 0:8], [t["s"]], [SC])
            P.op(V, lambda e: e.memset(COS[:, :, 0:1], 1.0), w=[COS])
            P.op(V, lambda e: e.memset(SIN[:, :, 0:1], 0.0), w=[SIN])
            ck, sk, ck2, sk2, nsk = t["c"], t["s"], t["dt"], t["lr"], t["tmp"]
            n = 1
            while n < CH:
                ts(V, nsk[:, 0:8], sk[:, 0:8], -1.0, None, ALU.mult, None, [sk], [nsk])
                for j in range(8):
                    tmp = t["ang"]
                    ts(V, tmp[:, 0:n], COS[:, j, 0:n], ck[:, j:j + 1], None, ALU.mult, None, [COS, ck], [tmp])
                    stt(COS[:, j, n:2 * n], SIN[:, j, 0:n], nsk[:, j:j + 1], tmp[:, 0:n], ALU.mult, ALU.add,
                        [SIN, nsk, tmp], [COS])
                    ts(V, tmp[:, 0:n], SIN[:, j, 0:n], ck[:, j:j + 1], None, ALU.mult, None, [SIN, ck], [tmp])
                    stt(SIN[:, j, n:2 * n], COS[:, j, 0:n], sk[:, j:j + 1], tmp[:, 0:n], ALU.mult, ALU.add,
                        [COS, sk, tmp], [SIN])
                tt(V, ck2[:, 0:8], ck[:, 0:8], ck[:, 0:8], ALU.mult, [ck], [ck2])
                tt(V, sk2[:, 0:8], sk[:, 0:8], sk[:, 0:8], ALU.mult, [sk], [sk2])
                tt(V, ck2[:, 0:8], ck2[:, 0:8], sk2[:, 0:8], ALU.subtract, [ck2, sk2], [ck2])
                tt(V, sk2[:, 0:8], sk[:, 0:8], ck[:, 0:8], ALU.mult, [sk, ck], [sk2])
                ts(V, sk2[:, 0:8], sk2[:, 0:8], 2.0, None, ALU.mult, None, [sk2], [sk2])
                ck, ck2 = ck2, ck
                sk, sk2 = sk2, sk
                n *= 2
            f32pool.put(*t.values())
            for half in range(2):
                sl = slice(half * 512, (half + 1) * 512)
                gl = P.new_group(f"s5par{half}")
                ld = [f32pool.get() for _ in range(7)]
                lre_t, lim_t, ldt_t, bre_t, bim_t, cre_t, cim_t = ld
                for tbuf, src in ((lre_t, s5rep_d[l, 0, :, sl]), (lim_t, s5rep_d[l, 1, :, sl]), (ldt_t, s5rep_d[l, 2, :, sl]),
                                  (bre_t, s5b_d[l, 0, :, sl]), (bim_t, s5b_d[l, 1, :, sl]),
                                  (cre_t, s5c_d[l, 0, :, sl]), (cim_t, s5c_d[l, 1, :, sl])):
                    tbuf.group = gl
                    P.dma("sync", tbuf[:], src, s5rep_d, tbuf)
                t = s5_a(lre_t[:], lim_t[:], ldt_t[:], 512, tiles=[lre_t, lim_t, ldt_t])
                mag, c_, s_, lr, tmp, ang, dtt = t["mag"], t["c"], t["s"], t["lr"], t["tmp"], t["ang"], t["dt"]
                tt(V, c_[:], c_[:], mag[:], ALU.mult, [c_, mag], [c_])
                tt(V, s_[:], s_[:], mag[:], ALU.mult, [s_, mag], [s_])
                ts(V, c_[:], c_[:], -1.0, None, ALU.add, None, [c_], [c_])
                tt(V, tmp[:], lr[:], lr[:], ALU.mult, [lr], [tmp])
                tt(V, ang[:], lim_t[:], lim_t[:], ALU.mult, [lim_t], [ang])
                tt(V, tmp[:], tmp[:], ang[:], ALU.add, [tmp, ang], [tmp])
                P.op(V, lambda e, o=tmp: e.reciprocal(out=o[:], in_=o[:]), r=[tmp], w=[tmp])
                gre, gim = mag, dtt
                tt(V, gre[:], c_[:], lr[:], ALU.mult, [c_, lr], [gre])
                tt(V, ang[:], s_[:], lim_t[:], ALU.mult, [s_, lim_t], [ang])
                tt(V, gre[:], gre[:], ang[:], ALU.add, [gre, ang], [gre])
                tt(V, gre[:], gre[:], tmp[:], ALU.mult, [gre, tmp], [gre])
                tt(V, gim[:], s_[:], lr[:], ALU.mult, [s_, lr], [gim])
                tt(V, ang[:], c_[:], lim_t[:], ALU.mult, [c_, lim_t], [ang])
                tt(V, gim[:], gim[:], ang[:], ALU.subtract, [gim, ang], [gim])
                tt(V, gim[:], gim[:], tmp[:], ALU.mult, [gim, tmp], [gim])
                tt(V, ang[:], gre[:], bre_t[:], ALU.mult, [gre, bre_t], [ang])
                tt(V, tmp[:], gim[:], bim_t[:], ALU.mult, [gim, bim_t], [tmp])
                tt(V, BRE[:, 4 * half:4 * half + 4, :], ang[:].rearrange("p (j m) -> p j m", m=128),
                   tmp[:].rearrange("p (j m) -> p j m", m=128), ALU.subtract, [ang, tmp], [BRE])
                tt(V, ang[:], gre[:], bim_t[:], ALU.mult, [gre, bim_t], [ang])
                tt(V, tmp[:], gim[:], bre_t[:], ALU.mult, [gim, bre_t], [tmp])
                tt(V, BIM[:, 4 * half:4 * half + 4, :], ang[:].rearrange("p (j m) -> p j m", m=128),
                   tmp[:].rearrange("p (j m) -> p j m", m=128), ALU.add, [ang, tmp], [BIM])
                cp(V, CRE[:, 4 * half:4 * half + 4, :], cre_t[:].rearrange("p (j m) -> p j m", m=128), [cre_t], [CRE])
                ts(V, CIMN[:, 4 * half:4 * half + 4, :], cim_t[:].rearrange("p (j m) -> p j m", m=128), -1.0, None,
                   ALU.mult, None, [cim_t], [CIMN])
                f32pool.put(*t.values())
                for b in ld:
                    b.group = None
                f32pool.put(*ld)

        def ck(name):
            if cfg.get("stop") == name:
                raise StopBuild()

        def chunk_layer(l, is_s, first, reset, out_slot, use_recv, x_off, y_off):
            S, L = (NS, LS) if is_s else (1, CH)
            last_p = (out_slot is not None) and not is_s
            t0 = x_off

            def v3(ap):
                return ap.rearrange("p (s l) -> p s l", l=L)

            P.dma(G, Xall[:, :, :], xT[:, :, t0:t0 + CH].rearrange("k p t -> p k t"), xT, X)
            if first:
                cast_weights(0, 1)
            if use_recv:
                rts = []
                for k in range(8):
                    rt = f32pool.get()
                    rt.group = grecv
                    P.dma(G, rt[:], recv[0:128, k * CH:(k + 1) * CH], recv, rt)
                    rts.append(rt)
                for k in range(8):
                    stt(X[k][:], rts[k][:], col("role"), X[k][:], ALU.mult, ALU.add, [rts[k], cols, X[k]], [X[k]])
                for rt in rts:
                    rt.group = None
                    f32pool.put(rt)
            if cfg.get("dbg") == 1:
                P.dma(G, y_raw[:, :, y_off:y_off + CH].rearrange("k p t -> p k t"), Xall[:, :, :], X, y_raw, primary=X[0])
            if reset:
                mo = col("m_own")
                for i in range(3):
                    ts(V, SST[i][:, 0, :], SST[i][:, 0, :], mo, None, ALU.mult, None, [SST[i], cols], [SST[i]])
                    ts(V, XRH[i][:, 0:3], XRH[i][:, 0:3], mo, None, ALU.mult, None, [XRH[i], cols], [XRH[i]])
                ts(V, H5[:, :, 0, :], H5[:, :, 0, :], mo, None, ALU.mult, None, [H5, cols], [H5])
                ts(V, HL[:, :, 0:1], HL[:, :, 0:1], mo, None, ALU.mult, None, [HL, cols], [HL])
            if is_s:
                for i in range(3):
                    P.dma(G, SIN_H[i][:, 0:8, :], st_hgrn[l, i], st_hgrn, SIN_H[i])
                P.dma(G, H5[:], st_s5[l], st_s5, H5)
                P.dma(G, HL[:], st_lru[l], st_lru, HL)
                P.dma(G, CVI[:], st_conv[l], st_conv, CVI)
            elif first:
                P.op(V, lambda e: e.memset(H5[:], 0.0), w=[H5])
                P.op(V, lambda e: e.memset(HL[:], 0.0), w=[HL])
                for i in range(3):
                    P.op(V, lambda e, i=i: e.memset(SST[i][:, 0, :], 0.0), w=[SST[i]])
                    P.op(V, lambda e, i=i: e.memset(XRH[i][:, 0:3], 0.0), w=[XRH[i]])

            def rmsnorm_to_bf16(wname):
                psn = psmm.get()
                sq = [b16pool.get() for _ in range(8)]
                for k in range(8):
                    act(sq[k][:], X[k][:], AF.Square, [X[k]], [sq[k]])
                for k in range(8):
                    mm(psn[:], ONES, sq[k][:], [cmat, sq[k]], [psn], start=(k == 0), stop=(k == 7), inc=(k == 7))
                b16pool.put(*sq)
                rs = f32pool.get()
                rstd_from(psn, D, rs)
                psmm.put(psn)
                xn = [b16pool.get() for _ in range(8)]
                for k in range(8):
                    stt(xn[k][:], X[k][:], col(wname, k), rs[:], ALU.mult, ALU.mult, [X[k], cols, rs], [xn[k]])
                f32pool.put(rs)
                return xn

            ck("load")
            xn = rmsnorm_to_bf16(f"norm_mix{l}")
            ck("norm")
            proj_ps = {}
            pan = None

            def proj_tile(f):
                nonlocal pan
                if f % 4 == 0:
                    pan = get_panel()
                slot, pv = pan
                ps = psmm.get()
                for k in range(8):
                    mm(ps[:], pv[:, k, (f % 4) * 128:(f % 4 + 1) * 128], xn[k][:], [slot, xn[k]], [ps],
                       start=(k == 0), stop=(k == 7), inc=(k == 7))
                return ps

            Q = []; SG = []; VF = []; GS = []; U = []; UB = []; GG = []
            for f in range(20):
                ps = proj_tile(f)
                if f < 3:
                    o = f32pool.get(); cp(A, o[:], ps[:], [ps], [o]); Q.append(o)
                elif f < 6:
                    o = f32pool.get(); act(o[:], ps[:], AF.Sigmoid, [ps], [o]); SG.append(o)
                elif f < 9:
                    o = b16pool.get(); cp(V, o[:], ps[:], [ps], [o]); VF.append(o)
                elif f < 12:
                    o = f32pool.get(); act(o[:], ps[:], AF.Silu, [ps], [o]); GS.append(o)
                elif f < 14:
                    o = f32pool.get(); cp(V, o[:], ps[:], [ps], [o]); U.append(o)
                    ob = b16pool.get(); cp(A, ob[:], ps[:], [ps], [ob]); UB.append(ob)
                elif f < 17:
                    i = f - 14
                    dst = XRH[i][:, 0:S * (L + 3)].rearrange("p (s l) -> p s l", l=L + 3)[:, :, 3:3 + L]
                    cp(V, dst, v3(ps[:]), [ps], [XRH[i]])
                else:
                    o = f32pool.get(); act(o[:], ps[:], AF.Gelu_apprx_tanh, [ps], [o]); GG.append(o)
                psmm.put(ps)
                ck(f"proj{f}")
            b16pool.put(*xn)
            ck("proj")
            MIX = [b16pool.get() for _ in range(8)]

            def hgrn_gen():
                QT = []; KT_ = []; QP = []; KPP = []; DEC = []
                for i in range(3):
                    F_ = SG[i]
                    act(F_[:], SG[i][:], AF.Identity, [SG[i], SC], [F_], scale=SC[:, 3 + i:4 + i], bias=SC[:, i:i + 1])
                    LF = f32pool.get()
                    act(LF[:], F_[:], AF.Ln, [F_], [LF])
                    yield
                    KK = F_
                    act(KK[:], F_[:], AF.Identity, [F_, EPSC], [KK], scale=-1.0, bias=EPSC[:, 1:2])
                    Gt = f32pool.get()
                    scan(Gt[:], RM[:], LF[:], 0.0, [RM, LF], [Gt])
                    yield
                    G3 = Gt[:].rearrange("p (s l) -> p s l", l=64)
                    D1 = LF
                    tt(V, D1[:].rearrange("p (s l) -> p s l", l=64), G3, G3[:, :, 31:32].to_broadcast([128, 8, 64]),
                       ALU.subtract, [Gt], [D1])
                    D2 = f32pool.get()
                    tt(V, D2[:].rearrange("p (s l) -> p s l", l=64), G3, G3[:, :, 63:64].to_broadcast([128, 8, 64]),
                       ALU.subtract, [Gt], [D2])
                    yield
                    EQ = f32pool.get(); EK = f32pool.get(); EG = f32pool.get()
                    act(EQ[:], D1[:], AF.Exp, [D1], [EQ])
                    act(EK[:], D1[:], AF.Exp, [D1], [EK], scale=-1.0)
                    act(EG[:], Gt[:], AF.Exp, [Gt], [EG])
                    act(D2[:], D2[:], AF.Exp, [D2], [D2], scale=-1.0)
                    dec = DECT[i]
                    act(dec[:, 0:8], G3[:, :, 63], AF.Exp, [Gt], [dec])
                    yield
                    qt = b16pool.get()
                    tt(V, qt[:], Q[i][:], EQ[:], ALU.mult, [Q[i], EQ], [qt])
                    kt = b16pool.get()
                    tt(V, kt[:], KK[:], EK[:], ALU.mult, [KK, EK], [kt])
                    yield
                    qp = EG
                    tt(V, qp[:], Q[i][:], EG[:], ALU.mult, [Q[i], EG], [qp])
                    kpp = b16pool.get()
                    tt(V, kpp[:], KK[:], D2[:], ALU.mult, [KK, D2], [kpp])
                    f32pool.put(LF, Gt, D2, EQ, EK, F_, Q[i])
                    QT.append(qt); KT_.append(kt); QP.append(qp); KPP.append(kpp); DEC.append(dec)
                    yield
                ck("hgrn_ew")
                VT = []; KTOK = []
                for (srcs, dstl) in ((VF, VT), (KPP, KTOK)):
                    for tb in range(4):
                        for i in range(3):
                            P.op(T, lambda e, i=i, tb=tb, srcs=srcs: e.transpose(pst[:, i * 128:(i + 1) * 128], srcs[i][:, tb * 128:(tb + 1) * 128], IDENT),
                                 r=[srcs[i], cmat], w=[pst])
                        o = b16pool.get()
                        cp(A, o[:, 0:384], pst[:, 0:384], [pst], [o])
                        dstl.append(o)
                        yield
                b16pool.put(*VF); b16pool.put(*KPP)
                ck("hgrn_tr")
                ATT = []
                for h in range(6):
                    i, hb = h // 2, (h % 2) * 64
                    psa = ps_hg.get()
                    for sc in range(8):
                        pb = (sc % 2) * 64
                        cs = slice(sc * 64, (sc + 1) * 64)
                        oc = slice((sc // 2) * 64, (sc // 2 + 1) * 64)
                        mm(psa[pb:pb + 64, oc], KT_[i][hb:hb + 64, cs], QT[i][hb:hb + 64, cs], [KT_[i], QT[i]], [psa],
                           start=(sc < 2), stop=True, inc=(sc == 7))
                    if h % 2 == 0:
                        a = b16pool.get()
                        ATT.append(a)
                    a = ATT[i]
                    ao = (h % 2) * 256
                    tt(V, a[:, ao:ao + 256].rearrange("p (s l) -> p s l", l=64), psa[:, 0:256].rearrange("p (s l) -> p s l", l=64),
                       MASK[:].unsqueeze(1).to_broadcast([128, 4, 64]), ALU.mult, [psa, MASK], [a])
                    ps_hg.put(psa)
                    yield
                b16pool.put(*QT); b16pool.put(*KT_)
                ck("hgrn_att")
                for i in range(3):
                    pso = ps_hg.get()
                    for par in range(2):
                        pb = par * 64
                        for hh in range(2):
                            h = 2 * i + hh
                            for sc in range(par, 8, 2):
                                mm(pso[hh * 64:hh * 64 + 64, sc * 64:(sc + 1) * 64],
                                   VT[sc // 2][pb:pb + 64, h * 64:(h + 1) * 64],
                                   ATT[i][pb:pb + 64, hh * 256 + (sc // 2) * 64:hh * 256 + (sc // 2 + 1) * 64],
                                   [VT[sc // 2], ATT[i]], [pso], start=(par == 0 and sc == 0), stop=False,
                                   inc=(hh == 1 and sc >= 6))
                        P.pe_fence()
                        yield
                    stile = SIN_H[i] if is_s else SST[i]
                    for sc in range(8):
                        pb = (sc % 2) * 64
                        pss = ps_hg.get()
                        mm(pss[:, 0:128], KTOK[sc // 2][pb:pb + 64, i * 128:(i + 1) * 128],
                           VT[sc // 2][pb:pb + 64, i * 128:(i + 1) * 128], [KTOK[sc // 2], VT[sc // 2]], [pss])
                        so = sc if is_s else sc + 1
                        mm(pso[:, sc * 64:(sc + 1) * 64], stile[:, sc, :], QP[i][:, sc * 64:(sc + 1) * 64],
                           [stile, QP[i]], [pso], start=False, stop=True, inc=True)
                        for hh in range(2):
                            hs = slice(hh * 64, hh * 64 + 64)
                            stt(stile[hs, so, hs], stile[hs, sc, hs], DEC[i][hs, sc:sc + 1], pss[hs, hs],
                                ALU.mult, ALU.add, [stile, DEC[i], pss], [stile])
                        ps_hg.put(pss)
                        yield
                    if is_s:
                        P.dma(G, hgrnS[out_slot, i], SIN_H[i][:, 0:8, :], SIN_H[i], hgrnS, primary=GHG[i])
                    else:
                        if last_p:
                            P.dma(G, hgrnP[out_slot, i], SST[i][:, 8, :], SST[i], hgrnP, primary=GHG[i])
                        cp(V, SST[i][:, 0, :], SST[i][:, 8, :], [SST[i]], [SST[i]])
                    O = f32pool.get()
                    cp(A, O[:], pso[:], [pso], [O])
                    o2 = b16pool.get()
                    act(o2[:], pso[:], AF.Square, [pso], [o2])
                    ps_hg.put(pso)
                    psn = ps_hg.get()
                    mm(psn[:], BD64, o2[:], [cmat, o2], [psn])
                    b16pool.put(o2)
                    rs = f32pool.get()
                    rstd_from(psn, 64, rs)
                    ps_hg.put(psn)
                    stt(O[:], O[:], col(f"hgrn_norm{l}", i), rs[:], ALU.mult, ALU.mult, [O, cols, rs], [O])
                    tt(V, MIX[i][:], O[:], GS[i][:], ALU.mult, [O, GS[i]], [MIX[i]])
                    f32pool.put(O, rs, GS[i], QP[i])
                    yield
                b16pool.put(*ATT); b16pool.put(*VT); b16pool.put(*KTOK)

                yield
            def s5_gen():
                def tab(Tb, j):
                    if is_s:
                        return Tb[:, j, 0:LS].unsqueeze(1).to_broadcast([128, NS, LS])
                    return Tb[:, j, :].unsqueeze(1)
                for j in range(8):
                    cth, sth = SC[:, 17 + j:18 + j], SC[:, 25 + j:26 + j]
                    hr, hi = H5[:, j, 0:S, 0], H5[:, j, 0:S, 1]
                    tmp = f32pool.get()
                    ts(V, tmp[:, 0:S], hr, cth, None, ALU.mult, None, [H5, SC], [tmp])
                    ts(V, tmp[:, 8:8 + S], hi, sth, -1.0, ALU.mult, ALU.mult, [H5, SC], [tmp])
                    tt(V, G5[:, j, 0:S, 0], tmp[:, 0:S], tmp[:, 8:8 + S], ALU.add, [tmp], [G5])
                    ts(V, tmp[:, 0:S], hr, sth, None, ALU.mult, None, [H5, SC], [tmp])
                    ts(V, tmp[:, 8:8 + S], hi, cth, None, ALU.mult, None, [H5, SC], [tmp])
                    tt(V, G5[:, j, 0:S, 1], tmp[:, 0:S], tmp[:, 8:8 + S], ALU.add, [tmp], [G5])
                    yield
                    f32pool.put(tmp)
                Y5 = []
                for j in range(8):
                    kt = j // 4
                    pbr = ps_s5.get(); pbi = ps_s5.get()
                    mm(pbr[:], BRE[:, j, :], UB[kt][:], [BRE, UB[kt]], [pbr])
                    mm(pbi[:], BIM[:, j, :], UB[kt][:], [BIM, UB[kt]], [pbi])
                    cs_, sn_ = tab(COS, j), tab(SIN, j)
                    t1 = f32pool.get(); t2 = f32pool.get(); gr = f32pool.get(); gi = f32pool.get()
                    tt(V, v3(t1[:]), v3(pbr[:]), cs_, ALU.mult, [pbr, COS], [t1])
                    tt(V, v3(t2[:]), v3(pbi[:]), sn_, ALU.mult, [pbi, SIN], [t2])
                    tt(V, v3(gr[:]), v3(pbi[:]), cs_, ALU.mult, [pbi, COS], [gr])
                    tt(V, v3(gi[:]), v3(pbr[:]), sn_, ALU.mult, [pbr, SIN], [gi])
                    tt(V, t1[:], t1[:], t2[:], ALU.add, [t1, t2], [t1])
                    tt(V, t2[:], gr[:], gi[:], ALU.subtract, [gr, gi], [t2])
                    ps_s5.put(pbr, pbi)
                    yield
                    magb = SC[:, 9 + j:10 + j]
                    for s in range(S):
                        sl = slice(s * L, (s + 1) * L)
                        scan(gr[:, sl], magb.to_broadcast([128, L]), t1[:, sl], G5[:, j, s, 0:1], [SC, t1, G5], [gr])
                        scan(gi[:, sl], magb.to_broadcast([128, L]), t2[:, sl], G5[:, j, s, 1:2], [SC, t2, G5], [gi])
                        yield
                    hr32 = t1; hi32 = t2
                    t3 = f32pool.get()
                    tt(V, v3(hr32[:]), v3(gr[:]), cs_, ALU.mult, [gr, COS], [hr32])
                    tt(V, v3(t3[:]), v3(gi[:]), sn_, ALU.mult, [gi, SIN], [t3])
                    tt(V, v3(hi32[:]), v3(gr[:]), sn_, ALU.mult, [gr, SIN], [hi32])
                    tt(V, v3(gi[:]), v3(gi[:]), cs_, ALU.mult, [gi, COS], [gi])
                    yield
                    tt(V, hr32[:], hr32[:], t3[:], ALU.subtract, [hr32, t3], [hr32])
                    tt(V, hi32[:], hi32[:], gi[:], ALU.add, [hi32, gi], [hi32])
                    hrb = b16pool.get(); hib = b16pool.get()
                    cp(A, hrb[:], hr32[:], [hr32], [hrb])
                    cp(A, hib[:], hi32[:], [hi32], [hib])
                    dstH = H5O if is_s else H5
                    cp(V, dstH[:, j, 0:S, 0], v3(hr32[:])[:, :, L - 1], [hr32], [dstH])
                    cp(V, dstH[:, j, 0:S, 1], v3(hi32[:])[:, :, L - 1], [hi32], [dstH])
                    f32pool.put(t1, t2, t3, gr, gi)
                    if j % 4 == 0:
                        psy = ps_s5.get()
                    mm(psy[:], CRE[:, j, :], hrb[:], [CRE, hrb], [psy], start=(j % 4 == 0), stop=False, inc=False)
                    mm(psy[:], CIMN[:, j, :], hib[:], [CIMN, hib], [psy], start=False, stop=(j % 4 == 3), inc=True)
                    b16pool.put(hrb, hib)
                    if j % 4 == 3:
                        kt = j // 4
                        y = U[kt]
                        stt(y[:], U[kt][:], col(f"s5_d{l}", kt), psy[:], ALU.mult, ALU.add, [U[kt], cols, psy], [y])
                        ps_s5.put(psy)
                        z = f32pool.get()
                        act(z[:], y[:], AF.Gelu_apprx_tanh, [y], [z])
                        zb = b16pool.get()
                        cp(A, zb[:], z[:], [z], [zb])
                        f32pool.put(y)
                        Y5.append((z, zb))
                    yield
                if is_s:
                    P.dma(G, s5S[out_slot], H5O[:], H5O, s5S)
                elif last_p:
                    cp(V, H5O[:, :, 0, :], H5[:, :, 0, :], [H5], [H5O])
                    P.dma(G, s5P[out_slot], H5O[:, :, 0, :], H5O, s5P)
                Z2 = []
                b16pool.put(*UB)
                psn = ps_s5.get()
                for mt in range(2):
                    psg = ps_s5.get()
                    for kt in range(2):
                        mm(psg[:], WGLU[:, kt, mt * 128:(mt + 1) * 128], Y5[kt][1][:], [WGLU, Y5[kt][1]], [psg],
                           start=(kt == 0), stop=(kt == 1), inc=(kt == 1))
                    sg = f32pool.get()
                    act(sg[:], psg[:], AF.Sigmoid, [psg, cols], [sg], bias=col(f"s5_bglu{l}", mt))
                    ps_s5.put(psg)
                    z2 = Y5[mt][0]
                    tt(V, z2[:], Y5[mt][0][:], sg[:], ALU.mult, [z2, sg], [z2])
                    f32pool.put(sg)
                    zsq = b16pool.get()
                    act(zsq[:], z2[:], AF.Square, [z2], [zsq])
                    mm(psn[:], ONES, zsq[:], [cmat, zsq], [psn], start=(mt == 0), stop=(mt == 1), inc=(mt == 1))
                    b16pool.put(zsq)
                    Z2.append(z2)
                    yield
                b16pool.put(Y5[0][1], Y5[1][1])
                rs = f32pool.get()
                rstd_from(psn, DB, rs)
                ps_s5.put(psn)
                for mt in range(2):
                    stt(MIX[3 + mt][:], Z2[mt][:], col(f"s5_norm{l}", mt), rs[:], ALU.mult, ALU.mult, [Z2[mt], cols, rs], [MIX[3 + mt]])
                f32pool.put(rs, *Z2)

                yield
            def lru_gen():
                YL = []
                psn = ps_lru.get()
                for i in range(3):
                    xrh = XRH[i][:, 0:S * (L + 3)].rearrange("p (s l) -> p s l", l=L + 3)
                    if is_s:
                        cp(V, xrh[:, :, 0:3], CVI[:, i, :, :], [CVI], [XRH[i]])
                    xc = f32pool.get()
                    act(v3(xc[:]), xrh[:, :, 3:3 + L], AF.Identity, [XRH[i], cols], [xc],
                        scale=col(f"conv_w{l}_3", i), bias=col(f"conv_b{l}", i))
                    for jj in range(3):
                        for s in range(S):
                            stt(xc[:, s * L:(s + 1) * L], xrh[:, s, jj:jj + L], col(f"conv_w{l}_{jj}", i), xc[:, s * L:(s + 1) * L],
                                ALU.mult, ALU.add, [XRH[i], cols, xc], [xc])
                            yield
                    if is_s:
                        cp(V, CVO[:, i, :, :], xrh[:, :, L:L + 3], [XRH[i]], [CVO])
                    else:
                        if last_p:
                            cp(V, CVO[:, i, 0, :], xrh[:, 0, L:L + 3], [XRH[i]], [CVO])
                        tmpc = f32pool.get()
                        cp(V, tmpc[:, 0:3], xrh[:, 0, L:L + 3], [XRH[i]], [tmpc])
                        cp(V, XRH[i][:, 0:3], tmpc[:, 0:3], [tmpc], [XRH[i]])
                        f32pool.put(tmpc)
                    r_ = f32pool.get(); i_ = f32pool.get()
                    psa_ = ps_lru.get()
                    mm(psa_[:], LRUW[:, 0, i, :], xc[:], [LRUW, xc], [psa_])
                    act(r_[:], psa_[:], AF.Sigmoid, [psa_, cols], [r_], bias=col(f"lru_ba{l}", i))
                    ps_lru.put(psa_)
                    yield
                    psx_ = ps_lru.get()
                    mm(psx_[:], LRUW[:, 1, i, :], xc[:], [LRUW, xc], [psx_])
                    act(i_[:], psx_[:], AF.Sigmoid, [psx_, cols], [i_], bias=col(f"lru_bx{l}", i))
                    ps_lru.put(psx_)
                    yield
                    a_ = r_
                    act(a_[:], r_[:], AF.Exp, [r_, SC], [a_], scale=SC[:, 6 + i:7 + i])
                    sq_ = f32pool.get()
                    act(sq_[:], a_[:], AF.Square, [a_], [sq_])
                    act(sq_[:], sq_[:], AF.Sqrt, [sq_], [sq_], scale=-1.0, bias=EPSC[:, 1:2])
                    yield
                    tt(V, i_[:], i_[:], xc[:], ALU.mult, [i_, xc], [i_])
                    tt(V, i_[:], i_[:], sq_[:], ALU.mult, [i_, sq_], [i_])
                    h_ = xc
                    dstH = HLO if is_s else HL
                    for s in range(S):
                        sl = slice(s * L, (s + 1) * L)
                        scan(h_[:, sl], a_[:, sl], i_[:, sl], HL[:, i, s:s + 1], [a_, i_, HL], [h_])
                        yield
                    tmph = sq_
                    cp(V, tmph[:, 0:S], v3(h_[:])[:, :, L - 1], [h_], [tmph])
                    cp(V, dstH[:, i, 0:S], tmph[:, 0:S], [tmph], [dstH])
                    y_ = i_
                    tt(V, y_[:], h_[:], GG[i][:], ALU.mult, [h_, GG[i]], [y_])
                    ysq = b16pool.get()
                    act(ysq[:], y_[:], AF.Square, [y_], [ysq])
                    mm(psn[:], ONES, ysq[:], [cmat, ysq], [psn], start=(i == 0), stop=(i == 2), inc=(i == 2))
                    b16pool.put(ysq)
                    f32pool.put(a_, sq_, h_, GG[i])
                    YL.append(y_)
                    yield
                if is_s:
                    P.dma(G, lruS[out_slot], HLO[:], HLO, lruS)
                    P.dma(G, convS[out_slot], CVO[:], CVO, convS)
                elif last_p:
                    cp(V, HLO[:, :, 0:1], HL[:, :, 0:1], [HL], [HLO])
                    P.dma(G, lruP[out_slot], HLO[:, :, 0], HLO, lruP, allow_slow_non_contiguous=True)
                    P.dma(G, convP[out_slot], CVO[:, :, 0, :], CVO, convP)
                rs = f32pool.get()
                rstd_from(psn, DC, rs)
                ps_lru.put(psn)
                for i in range(3):
                    stt(MIX[5 + i][:], YL[i][:], col(f"lru_norm{l}", i), rs[:], ALU.mult, ALU.mult, [YL[i], cols, rs], [MIX[5 + i]])
                f32pool.put(rs, *YL)

                yield
            gens = [hgrn_gen(), s5_gen(), lru_gen()]
            if cfg.get('serial_mixers'):
                for g_ in gens:
                    for _ in g_:
                        pass
            else:
                prio = cfg.get("prio", (1, 1, 1))
                gl_ = list(zip(gens, prio))
                while gl_:
                    for (g_, n_) in list(gl_):
                        try:
                            for _ in range(n_):
                                next(g_)
                        except StopIteration:
                            gl_.remove((g_, n_))

            ck("lru")
            for f in range(8):
                if f % 4 == 0:
                    pan = get_panel()
                slot, pv = pan
                ps = psmm.get()
                for k in range(8):
                    mm(ps[:], pv[:, k, (f % 4) * 128:(f % 4 + 1) * 128], MIX[k][:], [slot, MIX[k]], [ps],
                       start=(k == 0), stop=(k == 7), inc=(k == 7))
                tt(V, X[f][:], X[f][:], ps[:], ALU.add, [X[f], ps], [X[f]])
                psmm.put(ps)
            b16pool.put(*MIX)

            ck("wout")
            xn = [b16pool.get() for _ in range(8)]
            for k in range(8):
                act(xn[k][:], X[k][:], AF.Identity, [X[k], cols], [xn[k]], scale=col(f"norm_ffn{l}", k))
            psn = psmm.get()
            sq = [b16pool.get() for _ in range(8)]
            for k in range(8):
                tt(V, sq[k][:], X[k][:], X[k][:], ALU.mult, [X[k]], [sq[k]])
            for k in range(8):
                mm(psn[:], ONES, sq[k][:], [cmat, sq[k]], [psn], start=(k == 0), stop=(k == 7), inc=(k == 7))
            b16pool.put(*sq)
            rsf = f32pool.get()
            rstd_from(psn, D, rsf)
            psmm.put(psn)
            HID = []
            for j in range(6):
                nt = 4 if j < 5 else 2
                sg_, pg = get_panel()
                su_, pu = get_panel(hold=1)
                for q in range(nt):
                    psg = psmm.get(); psu = psmm.get()
                    for k in range(8):
                        mm(psg[:], pg[:, k, q * 128:(q + 1) * 128], xn[k][:], [sg_, xn[k]], [psg],
                           start=(k == 0), stop=(k == 7), inc=(k == 7))
                    for k in range(8):
                        mm(psu[:], pu[:, k, q * 128:(q + 1) * 128], xn[k][:], [su_, xn[k]], [psu],
                           start=(k == 0), stop=(k == 7), inc=(k == 7))
                    g1 = f32pool.get(); u1 = f32pool.get()
                    tt(V, g1[:], psg[:], rsf[:], ALU.mult, [psg, rsf], [g1])
                    act(g1[:], g1[:], AF.Silu, [g1], [g1])
                    tt(V, u1[:], psu[:], rsf[:], ALU.mult, [psu, rsf], [u1])
                    psmm.put(psg, psu)
                    hb = b16pool.get()
                    tt(V, hb[:], g1[:], u1[:], ALU.mult, [g1, u1], [hb])
                    f32pool.put(g1, u1)
                    HID.append(hb)
            f32pool.put(rsf)
            b16pool.put(*xn)
            for f in range(8):
                pan = get_panel()
                slot, pv = pan
                ps = psmm.get()
                for k in range(NFF):
                    mm(ps[:], pv[:, k, 0:128], HID[k][:], [slot, HID[k]], [ps],
                       start=(k == 0), stop=(k == NFF - 1), inc=(k == NFF - 1))
                tt(V, X[f][:], X[f][:], ps[:], ALU.add, [X[f], ps], [X[f]])
                psmm.put(ps)
            b16pool.put(*HID)

            ck("ffn")
            P.dma(G, send[:], Xall[:, :, :].rearrange("p k t -> p (k t)"), X, send, primary=X[0])
            P.coll(lambda e: e.collective_compute("AllGather", ALU.bypass, replica_groups=[[0, 1], [2, 3], [4, 5], [6, 7]],
                                                  ins=[send[:].opt()], outs=[recv[:].opt()]), [send], [recv])
            if cfg.get("dbg"):
                if cfg.get("dbg") == 2:
                    P.dma(G, y_raw[:, :, y_off:y_off + CH].rearrange("k p t -> p k t"), Xall[:, :, :], X, y_raw, primary=X[0])
                return
            psn = psmm.get()
            sq = [b16pool.get() for _ in range(8)]
            for k in range(8):
                act(sq[k][:], X[k][:], AF.Square, [X[k]], [sq[k]])
            for k in range(8):
                mm(psn[:], ONES, sq[k][:], [cmat, sq[k]], [psn], start=(k == 0), stop=(k == 7), inc=(k == 7))
            b16pool.put(*sq)
            rs = f32pool.get()
            rstd_from(psn, D, rs)
            psmm.put(psn)
            for k in range(8):
                stt(X[k][:], X[k][:], col("norm_final", k), rs[:], ALU.mult, ALU.mult, [X[k], cols, rs], [X[k]])
            f32pool.put(rs)
            P.dma(G, y_raw[:, :, y_off:y_off + CH].rearrange("k p t -> p k t"), Xall[:, :, :], X, y_raw, primary=X[0])

        try:
            cast_weights(0, 0)
            ck("cast")
            layer_setup(0)
            ck("setup")
            for st in range(NST):
                chunk_layer(0, False, st == 0, st == 1, (st - (NPC - 1)) if st >= NPC - 1 else None, st >= 1,
                            st * CH, st * CH)
                ck("chunk")
            for t in range(2):
                chunk_layer(0, True, False, False, t, True, NST * CH, (NST + t) * CH)
        except StopBuild:
            pass
        P.finish("sync")
        P.emit()
        stats = dict(P.ninst)
        global _DBG_P
        _DBG_P = P
        stats["sems"] = P.nsem
    return nc, stats


def _colpack(inp, r):
    cols = np.zeros((128, NCOL), np.float32)

    def put(name, vec):
        vec = np.asarray(vec, np.float32).reshape(-1, 128).T
        o = COLS[name]
        cols[:, o:o + vec.shape[1]] = vec
    put("norm_mix0", inp["norm_mix"][r])
    put("norm_ffn0", inp["norm_ffn"][r])
    put("hgrn_lb0", inp["hgrn_lb"][0])
    put("hgrn_lb1", inp["hgrn_lb"][1])
    put("hgrn_norm0", inp["hgrn_norm"][r])
    put("s5_d0", inp["s5_d"][r])
    put("s5_bglu0", inp["s5_b_glu"][r])
    put("s5_norm0", inp["s5_norm"][r])
    for j in range(4):
        put(f"conv_w0_{j}", inp["lru_conv_w"][r, j])
    put("conv_b0", inp["lru_conv_b"][r])
    put("lru_ba0", inp["lru_ba"][r])
    put("lru_bx0", inp["lru_bx"][r])
    put("lru_lam0", inp["lru_lam"][r])
    put("lru_norm0", inp["lru_norm"][r])
    put("s5_lre0", np.asarray(inp["s5_lam_re"][r]).reshape(-1))
    put("s5_lim0", np.asarray(inp["s5_lam_im"][r]).reshape(-1))
    put("s5_ldt0", np.repeat(np.asarray(inp["s5_log_dt"][r]), 64))
    put("norm_final", inp["norm_final"])
    cols[:, COLS["role"]] = float(r)
    cols[:, COLS["m_own"]] = float(1 - r)
    return cols


def _shared_inputs(inp):
    f = lambda a: np.ascontiguousarray(np.asarray(a, np.float32))
    sh = {}
    for k, nm in (("w_in", "w_in"), ("w_out", "w_out"), ("w_g", "w_ffn_gate"), ("w_u", "w_ffn_up"), ("w_d", "w_ffn_down")):
        sh[k] = f(inp[nm])
    rep = np.zeros((DEPTH, 3, 128, 1024), np.float32)
    for l in range(DEPTH):
        rep[l, 0] = np.asarray(inp["s5_lam_re"][l]).reshape(1, 1024)
        rep[l, 1] = np.asarray(inp["s5_lam_im"][l]).reshape(1, 1024)
        rep[l, 2] = np.repeat(np.asarray(inp["s5_log_dt"][l]), 64).reshape(1, 1024)
    sh["s5rep"] = rep
    sb = np.zeros((DEPTH, 2, 128, 1024), np.float32)
    sc = np.zeros((DEPTH, 2, 128, 1024), np.float32)
    for l in range(DEPTH):
        for ri, (bn, cn) in enumerate((("s5_b_re", "s5_c_re"), ("s5_b_im", "s5_c_im"))):
            b = np.asarray(inp[bn][l])
            cc = np.asarray(inp[cn][l])
            for g in range(16):
                g8 = g % 8
                sb[l, ri, g8 * 16:(g8 + 1) * 16, g * 64:(g + 1) * 64] = b[g].T
                j, g2 = g // 2, g % 2
                sc[l, ri, g2 * 64:(g2 + 1) * 64, j * 128 + g8 * 16:j * 128 + (g8 + 1) * 16] = cc[g].T
    sh["s5b"] = sb
    sh["s5c"] = sc
    sh["wglu"] = f(np.asarray(inp["s5_w_glu"]).reshape(DEPTH, 2, 128, 256).transpose(0, 2, 1, 3))
    lw = np.zeros((DEPTH, 2, 128, 3, 128), np.float32)
    for l in range(DEPTH):
        for ax, nm in enumerate(("lru_wa", "lru_wx")):
            w = np.asarray(inp[nm][l])
            for h in range(6):
                i, h2 = h // 2, h % 2
                lw[l, ax, h2 * 64:(h2 + 1) * 64, i, h2 * 64:(h2 + 1) * 64] = w[h]
    sh["lruw"] = lw
    cm = np.zeros((128, 3, 128), np.float32)
    cm[:, 0, :] = 1.0
    cm[0:64, 1, 0:64] = 1.0
    cm[64:128, 1, 64:128] = 1.0
    cm[:, 2, :] = np.eye(128, dtype=np.float32)
    sh["cmat"] = cm
    s_idx = np.arange(128)[:, None] % 64
    t_idx = np.arange(64)[None, :]
    sh["cmask"] = (s_idx <= t_idx).astype(np.float32)
    rm = np.ones((128, CH), np.float32)
    rm[:, ::64] = 0.0
    sh["crm"] = rm
    return sh


_LAYERED = ("s5rep", "s5b", "s5c", "wglu", "lruw")
_BIGW = ("w_in", "w_out", "w_g", "w_u", "w_d")


def _core_inputs(inp, sh, p, r, TP):
    m = {}
    for k in ("cmat", "cmask", "crm"):
        m[k] = sh[k]
    for k in _LAYERED:
        m[k] = sh[k] if r == 0 else np.ascontiguousarray(sh[k][::-1])
    for k in _BIGW:
        m[k] = np.ascontiguousarray(sh[k][r:r + 1])
    m["cols"] = _colpack(inp, r)
    ttx = TP + CH + NS * LS
    xcat = np.zeros((ttx, D), np.float32)
    if r == 0:
        xcat[:TP] = np.asarray(inp["x_prompt"][p, :TP], np.float32)
        xcat[TP + CH:] = np.asarray(inp["x_sample"][NS * p:NS * (p + 1)], np.float32).reshape(NS * LS, D)
    m["xT"] = np.ascontiguousarray(xcat.T.reshape(8, 128, -1))
    bs = slice(NS * p, NS * (p + 1))
    order = [r, 1 - r]
    sh_ = np.asarray(inp["state_hgrn"], np.float32)[order][:, bs]
    sh5 = sh_.reshape(DEPTH, NS, 3, 2, 64, 64)
    bd = np.zeros((DEPTH, 3, 2, 64, NS, 2, 64), np.float32)
    for h2 in range(2):
        bd[:, :, h2, :, :, h2, :] = sh5[:, :, :, h2].transpose(0, 2, 3, 1, 4)
    m["st_hgrn"] = np.ascontiguousarray(bd.reshape(DEPTH, 3, 128, NS, 128))
    sr = np.asarray(inp["state_s5_re"], np.float32)[order][:, bs].reshape(DEPTH, NS, 8, 128)
    si = np.asarray(inp["state_s5_im"], np.float32)[order][:, bs].reshape(DEPTH, NS, 8, 128)
    m["st_s5"] = np.ascontiguousarray(np.stack([sr, si], -1).transpose(0, 3, 2, 1, 4))
    sl = np.asarray(inp["state_rglru"], np.float32)[order][:, bs].reshape(DEPTH, NS, 3, 128)
    m["st_lru"] = np.ascontiguousarray(sl.transpose(0, 3, 2, 1))
    cv = np.asarray(inp["cache_conv"], np.float32)[order][:, bs].reshape(DEPTH, NS, 3, 3, 128)
    m["st_conv"] = np.ascontiguousarray(cv.transpose(0, 4, 3, 1, 2))
    return m


def _decode(r):
    o = {}
    a = np.asarray(r["hgrnP"]).reshape(2, 3, 2, 64, 2, 64)
    o["hg_p"] = np.stack([a[:, :, 0, :, 0, :], a[:, :, 1, :, 1, :]], 2).reshape(2, 6, 64, 64)
    a = np.asarray(r["hgrnS"]).reshape(2, 3, 2, 64, NS, 2, 64)
    a = np.stack([a[:, :, 0, :, :, 0, :], a[:, :, 1, :, :, 1, :]], 2)
    o["hg_s"] = a.transpose(0, 4, 1, 2, 3, 5).reshape(2, NS, 6, 64, 64)
    a = np.asarray(r["s5P"])
    o["s5r_p"] = a[..., 0].transpose(0, 2, 1).reshape(2, 16, 64)
    o["s5i_p"] = a[..., 1].transpose(0, 2, 1).reshape(2, 16, 64)
    a = np.asarray(r["s5S"])
    o["s5r_s"] = a[..., 0].transpose(0, 3, 2, 1).reshape(2, NS, 16, 64)
    o["s5i_s"] = a[..., 1].transpose(0, 3, 2, 1).reshape(2, NS, 16, 64)
    o["lru_p"] = np.asarray(r["lruP"]).transpose(0, 2, 1).reshape(2, DC)
    o["lru_s"] = np.asarray(r["lruS"]).transpose(0, 3, 2, 1).reshape(2, NS, DC)
    o["cv_p"] = np.asarray(r["convP"]).transpose(0, 3, 2, 1).reshape(2, 3, DC)
    o["cv_s"] = np.asarray(r["convS"]).transpose(0, 3, 4, 2, 1).reshape(2, NS, 3, DC)
    return o


_CACHE = {}


def run(inp, TP=SEQ, cfg=None, trace=False):
    key = (TP, str(cfg))
    if key not in _CACHE:
        _CACHE[key] = build_program(TP, cfg)
    nc, stats = _CACHE[key]
    sh = _shared_inputs(inp)
    in_maps = [_core_inputs(inp, sh, core // 2, core % 2, TP) for core in range(NCORE)]
    res = run_bass_kernel_spmd(nc, in_maps, core_ids=list(range(NCORE)), trace=trace)
    R = res.results
    B = NCORE // 2
    NST = TP // CH + 1
    y_p = np.zeros((B, TP, D), np.float32)
    y_s = np.zeros((B * NS, LS, D), np.float32)
    shapes = {"hg_p": (6, 64, 64), "s5r_p": (16, 64), "s5i_p": (16, 64), "lru_p": (DC,), "cv_p": (3, DC)}
    outp = {k: np.zeros((DEPTH, B) + v, np.float32) for k, v in shapes.items()}
    outs_ = {k[:-2] + "_s": np.zeros((DEPTH, B * NS) + v, np.float32) for k, v in shapes.items()}
    for p in range(B):
        dec = [_decode(R[2 * p]), _decode(R[2 * p + 1])]
        yt = np.asarray(R[2 * p + 1]["y_raw"]).reshape(D, -1).T
        y_p[p] = yt[CH:CH + TP]
        y_s[NS * p:NS * (p + 1)] = yt[(NST + 1) * CH:(NST + 2) * CH].reshape(NS, LS, D)
        bs = slice(NS * p, NS * (p + 1))
        for l in range(DEPTH):
            for k in outp:
                outp[k][l, p] = dec[l][k][l]
            for k in outs_:
                outs_[k][l, bs] = dec[l][k][l]
    outs = (y_p, y_s, outp["hg_p"], outp["s5r_p"], outp["s5i_p"], outp["lru_p"], outp["cv_p"],
            outs_["hg_s"], outs_["s5r_s"], outs_["s5i_s"], outs_["lru_s"], outs_["cv_s"])
    return outs, res


def kernel(**inputs):
    outs, _ = run(inputs)
    return outs
```
